# Optimizing a Trainium2 kernel written in Bass

```python
import jax, jax.numpy as jnp
from jax import lax
import numpy as np

D_MODEL = 4096
BATCH = 2
SEQ = 8192
DEPTH = 4

N_META = 16
CHUNK = 64
EPS = 1e-6
H_A = 4
V_A = D_MODEL // 4
QK_A = V_A // 2
DV_A = V_A // H_A
DK_A = QK_A // H_A
CONV_W = 4
F_BIAS = 3.0
H_B = 4
V_B = D_MODEL // 4
QK_B = V_B // 2
DV_B = V_B // H_B
DK_B = QK_B // H_B
GATE_RANK = 16
GATE_TAU = 16.0
D_FF = 4 * D_MODEL
SPLITS = (QK_A, QK_A, V_A, V_A, H_A, H_A, QK_B, QK_B, V_B, V_B, GATE_RANK, D_MODEL, D_MODEL)
P_IN = int(sum(SPLITS))
SPLIT_POINTS = tuple(int(s) for s in np.cumsum(SPLITS)[:-1])

kernel_name = 'mlstm_gla_griffin_merge_hybrid'


def rmsnorm(x, g):
    xf = x.astype(jnp.float32)
    y = xf * lax.rsqrt(jnp.mean(xf * xf, axis=-1, keepdims=True) + EPS)
    return (y * g.astype(jnp.float32)).astype(x.dtype)


def causal_dwconv(x, w):
    return lax.conv_general_dilated(x, w[:, None, :].astype(x.dtype), window_strides=(1,), padding=[(CONV_W - 1, 0)], dimension_numbers=('NWC', 'WIO', 'NWC'), feature_group_count=x.shape[-1])


def to_chunks(a):
    bsz, t = a.shape[0], a.shape[1]
    a = a.reshape((bsz, t // CHUNK, CHUNK) + a.shape[2:])
    return a.transpose((1, 0, 3, 2) + tuple(range(4, a.ndim)))


def from_chunks(y):
    nc, bsz, heads, l, d = y.shape
    return y.transpose(1, 0, 3, 2, 4).reshape(bsz, nc * l, heads, d)


def mlstm_chunked(q, k, v, i_pre, f_pre, valid):
    bsz, _, heads, dk = q.shape
    dv = v.shape[-1]
    causal = jnp.tril(jnp.ones((CHUNK, CHUNK), dtype=bool))
    log_i = jnp.where(valid[None, :, None], i_pre, -jnp.inf)
    log_f = jnp.where(valid[None, :, None], jax.nn.log_sigmoid(f_pre), 0.0)

    def step(carry, inp):
        c_st, n_st, m_st = carry
        qc, kc, vc, li, lf = inp
        b = jnp.cumsum(lf, axis=-1)
        log_d = jnp.where(causal, b[..., :, None] - b[..., None, :] + li[..., None, :], -jnp.inf)
        m_t = jnp.maximum(b + m_st[..., None], jnp.max(log_d, axis=-1))
        s = jnp.einsum('bhtd,bhsd->bhts', qc, kc) * jnp.exp(log_d - m_t[..., None])
        carry_w = jnp.exp(b + m_st[..., None] - m_t)
        num = jnp.einsum('bhts,bhsv->bhtv', s, vc) + carry_w[..., None] * jnp.einsum('bhtd,bhvd->bhtv', qc, c_st)
        den = jnp.sum(s, axis=-1) + carry_w * jnp.einsum('bhtd,bhd->bht', qc, n_st)
        h = num / jnp.maximum(jnp.abs(den), jnp.exp(-m_t))[..., None]
        g = b[..., -1]
        log_w = g[..., None] - b + li
        m_new = jnp.maximum(g + m_st, jnp.max(log_w, axis=-1))
        w = jnp.exp(log_w - m_new[..., None])
        decay = jnp.exp(g + m_st - m_new)
        c_st = decay[..., None, None] * c_st + jnp.einsum('bhs,bhsv,bhsd->bhvd', w, vc, kc)
        n_st = decay[..., None] * n_st + jnp.einsum('bhs,bhsd->bhd', w, kc)
        return (c_st, n_st, m_new), h

    init = (jnp.zeros((bsz, heads, dv, dk), jnp.float32), jnp.zeros((bsz, heads, dk), jnp.float32), jnp.zeros((bsz, heads), jnp.float32))
    _, hs = lax.scan(step, init, (to_chunks(q), to_chunks(k), to_chunks(v), to_chunks(log_i), to_chunks(log_f)))
    return from_chunks(hs)


def gla_chunked(q, k, v, log_a):
    bsz, _, heads, dk = q.shape
    dv = v.shape[-1]
    causal = jnp.tril(jnp.ones((CHUNK, CHUNK), dtype=bool))

    def step(s_st, inp):
        qc, kc, vc, la = inp
        cum = jnp.cumsum(la, axis=2)
        rel = jnp.exp(jnp.where(causal[:, :, None], cum[:, :, :, None, :] - cum[:, :, None, :, :], -jnp.inf))
        att = jnp.einsum('bhtc,bhsc,bhtsc->bhts', qc, kc, rel)
        o = jnp.einsum('bhts,bhsv->bhtv', att, vc) + jnp.einsum('bhtc,bhcv->bhtv', qc * jnp.exp(cum), s_st)
        last = cum[:, :, -1:, :]
        s_st = jnp.exp(last[:, :, 0])[..., None] * s_st + jnp.einsum('bhsc,bhsv->bhcv', kc * jnp.exp(last - cum), vc)
        return s_st, o

    init = jnp.zeros((bsz, heads, dk, dv), jnp.float32)
    _, os_ = lax.scan(step, init, (to_chunks(q), to_chunks(k), to_chunks(v), to_chunks(log_a)))
    return from_chunks(os_)


def hybrid_mixer(h, w_in, conv_qk, b_if, w_gla_gate, b_gla_gate, norm_gla, w_br_a, w_br_b, w_out):
    bsz, n, _ = h.shape
    dt = h.dtype
    f32 = jnp.float32
    proj = h @ w_in
    qa, ka, va, oa, ia, fa, qb, kb, vb, gb, zb, ga, gbr = jnp.split(proj, SPLIT_POINTS, axis=-1)
    pad = CHUNK - N_META
    t_len = pad + n
    valid = jnp.arange(t_len) >= pad

    def prep(a, heads):
        a = jnp.pad(a.astype(f32), ((0, 0), (pad, 0), (0, 0)))
        return a.reshape(bsz, t_len, heads, -1)

    qka = jax.nn.silu(causal_dwconv(jnp.concatenate([qa, ka], axis=-1), conv_qk))
    qa, ka = jnp.split(qka, 2, axis=-1)
    i_pre = prep(ia + b_if[:H_A].astype(dt), H_A)[..., 0]
    f_pre = prep(fa + b_if[H_A:].astype(dt), H_A)[..., 0]
    h_a = mlstm_chunked(prep(qa, H_A), prep(ka, H_A) * (DK_A ** -0.5), prep(va, H_A), i_pre, f_pre, valid)
    y_a = h_a[:, pad:].reshape(bsz, n, V_A) * jax.nn.sigmoid(oa.astype(f32))

    log_a = jax.nn.log_sigmoid((zb @ w_gla_gate + b_gla_gate.astype(dt)).astype(f32)) / GATE_TAU
    o_b = gla_chunked(prep(qb, H_B) * (DK_B ** -0.5), prep(kb, H_B), prep(vb, H_B), prep(log_a, H_B))[:, pad:]
    o_b = o_b * lax.rsqrt(jnp.mean(o_b * o_b, axis=-1, keepdims=True) + EPS)
    y_b = o_b.reshape(bsz, n, V_B) * norm_gla.astype(f32) * jax.nn.silu(gb.astype(f32))

    merged = jax.nn.sigmoid(ga) * (y_a.astype(dt) @ w_br_a) + jax.nn.sigmoid(gbr) * (y_b.astype(dt) @ w_br_b)
    return merged @ w_out


def setup_inputs(seed: int = 0) -> dict:
    key = jax.random.key(seed)
    ks = jax.random.split(key, 20)

    def nrm(k, shape, scale):
        return jax.random.normal(k, shape, jnp.float32) * scale

    return {
        'x': nrm(ks[0], (BATCH, SEQ, D_MODEL), 1.0),
        'meta': nrm(ks[1], (N_META, D_MODEL), 1.0),
        'norm_mix': 1.0 + nrm(ks[2], (DEPTH, D_MODEL), 0.02),
        'w_in': nrm(ks[3], (DEPTH, D_MODEL, P_IN), D_MODEL ** -0.5),
        'conv_qk': nrm(ks[4], (DEPTH, CONV_W, 2 * QK_A), CONV_W ** -0.5),
        'b_if': jnp.concatenate([nrm(ks[5], (DEPTH, H_A), 0.1), F_BIAS + nrm(ks[6], (DEPTH, H_A), 0.1)], axis=-1),
        'w_gla_gate': nrm(ks[7], (DEPTH, GATE_RANK, QK_B), GATE_RANK ** -0.5),
        'b_gla_gate': nrm(ks[8], (DEPTH, QK_B), 0.1),
        'norm_gla': 1.0 + nrm(ks[9], (DEPTH, V_B), 0.02),
        'w_br_a': nrm(ks[10], (DEPTH, V_A, D_MODEL), V_A ** -0.5),
        'w_br_b': nrm(ks[11], (DEPTH, V_B, D_MODEL), V_B ** -0.5),
        'w_out': nrm(ks[12], (DEPTH, D_MODEL, D_MODEL), D_MODEL ** -0.5),
        'norm_mlp': 1.0 + nrm(ks[13], (DEPTH, D_MODEL), 0.02),
        'w_up': nrm(ks[14], (DEPTH, D_MODEL, D_FF), D_MODEL ** -0.5),
        'w_down': nrm(ks[15], (DEPTH, D_FF, D_MODEL), D_FF ** -0.5),
        'norm_final': 1.0 + nrm(ks[16], (D_MODEL,), 0.02),
    }


def reference(x, meta, norm_mix, w_in, conv_qk, b_if, w_gla_gate, b_gla_gate, norm_gla, w_br_a, w_br_b, w_out, norm_mlp, w_up, w_down, norm_final):
    bsz = x.shape[0]
    h = jnp.concatenate([jnp.broadcast_to(meta.astype(x.dtype)[None], (bsz, N_META, D_MODEL)), x], axis=1)
    for l in range(DEPTH):
        h = h + hybrid_mixer(rmsnorm(h, norm_mix[l]), w_in[l], conv_qk[l], b_if[l], w_gla_gate[l], b_gla_gate[l], norm_gla[l], w_br_a[l], w_br_b[l], w_out[l])
        u = jax.nn.relu(rmsnorm(h, norm_mlp[l]) @ w_up[l])
        h = h + (u * u) @ w_down[l]
    return rmsnorm(h[:, N_META:], norm_final)
```

```python
import math
import numpy as np
from contextlib import ExitStack
import concourse.bass as bass
import concourse.mybir as mybir
from concourse.bass_utils import run_bass_kernel_spmd

F32 = mybir.dt.float32
BF16 = mybir.dt.bfloat16
AF = mybir.ActivationFunctionType
ALU = mybir.AluOpType

H = 4
DK = 128
DV = 256
EPS = 1e-6
N_META = 16


class Cfg:
    def __init__(self, D=4096, NTOK=8208, DEPTH=4, TT=342, L=114, NCORES=2):
        self.D, self.NTOK, self.DEPTH, self.TT, self.L, self.NCORES = D, NTOK, DEPTH, TT, L, NCORES
        self.KC = D // 128
        self.NT = NTOK // TT
        assert NTOK % TT == 0 and TT % L == 0
        self.NCH = TT // L
        self.NSLAB = 48 + 11 * self.KC
        self.S_QA, self.S_KA, self.S_VA, self.S_OA = 0, 4, 8, 16
        self.S_QB, self.S_KB, self.S_VB, self.S_GB = 24, 28, 32, 40
        self.S_GA = 48
        self.S_GBR = 48 + self.KC
        self.S_OUT = 48 + 2 * self.KC
        self.S_UP = 48 + 3 * self.KC
        self.S_DN = 48 + 7 * self.KC
        self.P_G1 = 0
        self.P_G2 = self.P_G1 + DEPTH * self.KC
        self.P_GF = self.P_G2 + DEPTH * self.KC
        self.P_CONV = self.P_GF + self.KC
        self.P_NG = self.P_CONV + DEPTH * 32
        self.NP = self.P_NG + DEPTH * 8


class Agent:
    __slots__ = ("name", "sem", "step", "count")

    def __init__(self, name, sem, step):
        self.name, self.sem, self.step, self.count = name, sem, step, 0


class Res:
    __slots__ = ("name", "w", "r")

    def __init__(self, name):
        self.name, self.w, self.r = name, None, {}


class Tr:
    ENGS = ("pe", "act", "dve", "pool", "sp")

    def __init__(self, nc, stack):
        self.nc, self.stack = nc, stack
        self.ops = {e: [] for e in self.ENGS}
        self.seen = {e: {} for e in self.ENGS}
        self.agents = {}
        for e in ("pe", "act", "dve", "pool"):
            self.agent(e, 1)

    def agent(self, name, step):
        if name not in self.agents:
            sem = self.stack.enter_context(self.nc.semaphore("s_" + name))
            self.agents[name] = Agent(name, sem, step)
        return self.agents[name]

    def _waits(self, eng, reads, writes, me=None, acc=False):
        deps = {}
        for r in reads:
            if r.w is not None:
                a, c = r.w
                if deps.get(a, 0) < c:
                    deps[a] = c
        for w in writes:
            if w.w is not None:
                a, c = w.w
                if not (acc and a is me) and deps.get(a, 0) < c:
                    deps[a] = c
            for a, c in w.r.items():
                if deps.get(a, 0) < c:
                    deps[a] = c
        seen = self.seen[eng]
        waits = []
        for a, c in deps.items():
            if seen.get(a.name, 0) < c:
                seen[a.name] = c
                waits.append((a.sem, c))
        return waits

    def op(self, eng, fn, reads=(), writes=(), acc=False):
        ag = self.agents[eng]
        waits = self._waits(eng, reads, writes, ag, acc)
        ag.count += 1
        c = ag.count
        self.ops[eng].append((waits, fn, ag.sem, 1))
        for r in reads:
            r.r[ag] = c
        for w in writes:
            w.w = (ag, c)
            w.r = {}

    def dma(self, queue, stream, fn, reads=(), writes=()):
        st = self.agent(stream, 16)
        waits = self._waits(queue, reads, writes)
        st.count += 16
        c = st.count
        self.ops[queue].append((waits, fn, st.sem, 16))
        for r in reads:
            r.r[st] = c
        for w in writes:
            w.w = (st, c)
            w.r = {}

    def final_waits(self, eng):
        out = []
        for a in self.agents.values():
            if a.count > 0 and self.seen[eng].get(a.name, 0) < a.count:
                out.append((a.sem, a.count))
        return out

    def replay(self, eng_name, e, extra_waits=()):
        for waits, fn, sem, inc in self.ops[eng_name]:
            for s, v in waits:
                e.wait_ge(s, v)
            fn(e).then_inc(sem, inc)
        for s, v in extra_waits:
            e.wait_ge(s, v)


def build_program(cfg):
    D, KC, TT, L, NT, NCH, DEPTH = cfg.D, cfg.KC, cfg.TT, cfg.L, cfg.NT, cfg.NCH, cfg.DEPTH
    NSLAB = cfg.NSLAB
    nc = bass.Bass("TRN2", target_bir_lowering=False)

    xin = nc.dram_tensor("xin", [NT, 128, KC, TT], F32, kind="ExternalInput")
    wa_l = [nc.dram_tensor(f"wa{l}", [NSLAB, 128, KC, 128], F32, kind="ExternalInput") for l in range(DEPTH)]
    wb = nc.dram_tensor("wb", [DEPTH * KC, 128, 16, 128], F32, kind="ExternalInput")
    ws = nc.dram_tensor("ws", [DEPTH, 128, KC, 24], F32, kind="ExternalInput")
    pvec_d = nc.dram_tensor("pvec", [128, cfg.NP], F32, kind="ExternalInput")
    bif_d = nc.dram_tensor("bif", [8, DEPTH], F32, kind="ExternalInput")
    wg_d = nc.dram_tensor("wg", [16, DEPTH * 512], F32, kind="ExternalInput")
    bg_d = nc.dram_tensor("bgbc", [128, DEPTH * 512], F32, kind="ExternalInput")
    cst_d = nc.dram_tensor("consts", [128, 4 * 128], F32, kind="ExternalInput")
    out = nc.dram_tensor("out", [NT, 128, KC, TT], F32, kind="ExternalOutput")
    hT_d = nc.dram_tensor("hT_scr", [NT, 128, KC, TT], F32)
    PART = 200
    nparts = (NSLAB + PART - 1) // PART
    wab_parts = [[nc.dram_tensor(f"wab_{l}_{p}", [min(PART, NSLAB - p * PART), 128, KC, 128], BF16)
                  for p in range(nparts)] for l in range(DEPTH)]

    def wab_at(l, idx):
        return wab_parts[l][idx // PART][idx % PART]
    wbb = nc.dram_tensor("wbb", [DEPTH * KC, 128, 16, 128], BF16)
    wsb = nc.dram_tensor("wsb", [DEPTH, 128, KC, 24], BF16)

    stack = ExitStack()
    tr = Tr(nc, stack)

    def sb(name, shape, dt=F32):
        return stack.enter_context(nc.sbuf_tensor(name, shape, dt))

    r_hd = [Res(f"hd{t}") for t in range(NT)]
    r_od = Res("od")
    hT = sb("hT", [128, KC, TT]); r_hT = Res("hT")
    xg = sb("xg", [128, KC, TT], BF16); r_xg = Res("xg")
    a2 = sb("a2", [128, KC, TT], BF16); r_a2 = Res("a2")
    yT = sb("yT", [128, 16, TT], BF16); r_yT = Res("yT")
    sq = [sb(f"sq{i}", [128, TT], BF16) for i in range(2)]; r_sq = [Res(f"sq{i}") for i in range(2)]
    rstd = sb("rstd", [128, TT]); r_rstd = Res("rstd")
    NSLOT = 3
    slots = [sb(f"slot{i}", [128, max(KC, 16), 128], BF16) for i in range(NSLOT)]
    r_slots = [Res(f"slot{i}") for i in range(NSLOT)]
    wss = sb("wss", [128, KC, 24], BF16); r_wss = Res("wss")
    pvec = sb("pvec_sb", [128, cfg.NP]); r_par = Res("params")
    bif = sb("bif_sb", [8, DEPTH])
    wg = sb("wg_sb", [16, DEPTH * 512])
    bgbc = sb("bgbc_sb", [128, 512]); r_bg = Res("bgbc")
    cst = sb("cst_sb", [128, 4 * 128])
    IDENT = cst[:, 0:128]
    TRI = cst[:, 128:256]
    ONES = cst[:, 256:384]
    MASKNEG = cst[:, 384:512]
    ones_bf = sb("ones_bf", [128, 128], BF16)
    tmpA = [sb(f"tmpA{i}", [128, TT]) for i in range(4)]; r_tmpA = [Res(f"tmpA{i}") for i in range(4)]
    rawq = [sb(f"rawq{h}", [128, 3 + TT]) for h in range(H)]; r_rawq = [Res(f"rawq{h}") for h in range(H)]
    rawk = [sb(f"rawk{h}", [128, 3 + TT]) for h in range(H)]; r_rawk = [Res(f"rawk{h}") for h in range(H)]
    stg_names = ["va0", "va1", "oa0", "oa1", "qb", "kb", "vb0", "vb1", "gb0", "gb1"]
    stg = {n: sb("stg_" + n, [128, TT]) for n in stg_names}
    r_stg = {n: Res("stg_" + n) for n in stg_names}
    cacc = sb("cacc", [128, TT]); r_cacc = Res("cacc")
    qa_bf = sb("qa_bf", [128, TT], BF16); r_qabf = Res("qa_bf")
    ka_bf = sb("ka_bf", [128, TT], BF16); r_kabf = Res("ka_bf")
    ka_f = sb("ka_f", [128, TT]); r_kaf = Res("ka_f")
    ifs = sb("ifs", [8, TT]); r_ifs = Res("ifs")
    lsg = sb("lsg", [8, TT]); r_lsg = Res("lsg")
    ift = [sb(f"ift{i}", [8, TT]) for i in range(3)]; r_ift = Res("ift")
    zT = sb("zT", [16, TT]); r_zT = Res("zT")
    CT = [sb(f"CT{h}", [128, DV + 1]) for h in range(H)]; r_CT = [Res(f"CT{h}") for h in range(H)]
    CTb = [sb(f"CTb{h}", [128, DV + 1], BF16) for h in range(H)]; r_CTb = [Res(f"CTb{h}") for h in range(H)]
    ST = [sb(f"ST{h}", [128, DV]) for h in range(H)]; r_ST = [Res(f"ST{h}") for h in range(H)]
    STb = [sb(f"STb{h}", [128, DV], BF16) for h in range(H)]; r_STb = [Res(f"STb{h}") for h in range(H)]
    def ct(name, shape, dt=F32):
        return sb("c_" + name, shape, dt), Res("c_" + name)
    colsA, r_colsA = ct("colsA", [128, 16])
    g4, r_g4 = ct("g4", [128, 8])
    b4, r_b4 = ct("b4", [128, 12])
    lfbc, r_lfbc = ct("lfbc", [128, 128])
    Bm, r_Bm = ct("Bm", [128, 128])
    DTt, r_DT = ct("DT", [128, 128])
    STt, r_STt = ct("STt", [128, 128], BF16)
    vA, r_vA = ct("vA", [128, DV + 1], BF16)
    kw, r_kw = ct("kw", [128, 128], BF16)
    tmpc, r_tmpc = ct("tmpc", [128, DV + 1])
    nd, r_nd = ct("nd", [128, DV + 1])
    rr, r_rr = ct("rr", [128, 2])
    ha, r_ha = ct("ha", [128, DV])
    la, r_la = ct("la", [128, 128])
    lt1, r_lt1 = ct("lt1", [128, 128])
    lt2, r_lt2 = ct("lt2", [128, 128])
    lt3, r_lt3 = ct("lt3", [128, 128])
    lt4, r_lt4 = ct("lt4", [128, 128])
    lt5, _ = ct("lt5", [128, 128])
    lnsc, _ = ct("lnsc", [128, 1])
    cum_sb, r_cum = ct("cum_sb", [128, 128])
    ecT, r_ecT = ct("ecT", [128, 128])
    encT, r_encT = ct("encT", [128, 128])
    elast, r_elast = ct("elast", [128, 1])
    qt, r_qt = ct("qt", [128, 128], BF16)
    kt, r_kt = ct("kt", [128, 128], BF16)
    attT, r_attT = ct("attT", [128, 128], BF16)
    vB, r_vB = ct("vB", [128, DV], BF16)
    edec, r_edec = ct("edec", [128, 128])
    kd, r_kd = ct("kd", [128, 128], BF16)
    on, r_on = ct("on", [128, DV])
    ssb, r_ssb = ct("ssb", [128, 2])
    junk, r_junk = ct("junk", [128, DV])

    banks = [stack.enter_context(nc.psum_tensor(f"ps{i}", [128, 512], F32)) for i in range(8)]
    r_bank = [Res(f"ps{i}") for i in range(8)]
    gb_rot = [0]

    def gbank():
        i = gb_rot[0] % 4
        gb_rot[0] += 1
        return banks[i], r_bank[i]

    mb_rot = [0]

    def mbank():
        i = 4 + mb_rot[0] % 3
        mb_rot[0] += 1
        return banks[i], r_bank[i]

    SSB, r_SSB = banks[7], r_bank[7]

    def mm(ps_ap, r_ps, lhsT, rhs, start, stop, reads):
        tr.op("pe", lambda e: e.matmul(ps_ap, lhsT=lhsT, rhs=rhs, start=start, stop=stop),
              reads=reads, writes=[r_ps], acc=True)

    def act(out_ap, in_ap, func, reads, writes, bias=None, scale=None, accum_out=None):
        kw_ = {}
        if bias is not None:
            kw_["bias"] = bias
        if scale is not None:
            kw_["scale"] = scale
        if accum_out is not None:
            kw_["accum_out"] = accum_out
        tr.op("act", lambda e: e.activation(out=out_ap, in_=in_ap, func=func, **kw_), reads=reads, writes=writes)

    def tt(eng, out_ap, in0, in1, op, reads, writes):
        tr.op(eng, lambda e: e.tensor_tensor(out=out_ap, in0=in0, in1=in1, op=op), reads=reads, writes=writes)

    def ts(eng, out_ap, in0, s1, s2, op0, op1, reads, writes):
        if s2 is None:
            tr.op(eng, lambda e: e.tensor_scalar(out=out_ap, in0=in0, scalar1=s1, scalar2=None, op0=op0),
                  reads=reads, writes=writes)
        else:
            tr.op(eng, lambda e: e.tensor_scalar(out=out_ap, in0=in0, scalar1=s1, scalar2=s2, op0=op0, op1=op1),
                  reads=reads, writes=writes)

    def stt(eng, out_ap, in0, scalar, in1, op0, op1, reads, writes):
        tr.op(eng, lambda e: e.scalar_tensor_tensor(out=out_ap, in0=in0, scalar=scalar, in1=in1, op0=op0, op1=op1),
              reads=reads, writes=writes)

    def cp(eng, out_ap, in_ap, reads, writes):
        if eng == "act":
            tr.op("act", lambda e: e.copy(out=out_ap, in_=in_ap), reads=reads, writes=writes)
        else:
            tr.op(eng, lambda e: e.tensor_copy(out=out_ap, in_=in_ap), reads=reads, writes=writes)

    def memset(eng, ap, val, writes):
        tr.op(eng, lambda e: e.memset(ap, val), writes=writes)

    r_wl = [Res(f"wbf{l}") for l in range(DEPTH)]
    CG = 4

    def cast_layer(l):
        stn = f"cast{l}"
        mode = getattr(cfg, "castmode", 0)
        for i in range(NSLAB):
            a = l * NSLAB + i
            tr.dma("pool", stn, lambda e, a=a, l=l: e.dma_start(
                out=wab_at(l, a - l * NSLAB).rearrange("p k c -> p (k c)"), in_=wa_l[l][a - l * NSLAB].rearrange("p k c -> p (k c)")),
                writes=[r_wl[l]])
        if mode == 1:
            return
        for i in range(KC):
            a = l * KC + i
            tr.dma("pool", stn, lambda e, a=a: e.dma_start(
                out=wbb[a].rearrange("p k c -> p (k c)"), in_=wb[a].rearrange("p k c -> p (k c)")),
                writes=[r_wl[l]])
        if mode == 2:
            return
        tr.dma("pool", stn, lambda e: e.dma_start(
            out=wsb[l].rearrange("p k c -> p (k c)"), in_=ws[l].rearrange("p k c -> p (k c)")), writes=[r_wl[l]])

    slab_i = [0]

    def load_slab(l, dram_ap, kcn):
        i = slab_i[0] % NSLOT
        slab_i[0] += 1
        t, r = slots[i], r_slots[i]
        tr.dma("sp", f"w{i}", lambda e: e.dma_start(out=t[:, 0:kcn, :], in_=dram_ap), reads=[r_wl[l]], writes=[r])
        return t, r

    def gemm(l, slab_idx, act_t, r_act, kcn=None, ncols=128, small=None, wbslab=None):
        ps, r_ps = gbank()
        if small is not None:
            c0, ncols = small
            for kc in range(KC):
                mm(ps[0:ncols, 0:TT], r_ps, wss[:, kc, c0:c0 + ncols], act_t[:, kc, :], kc == 0, kc == KC - 1,
                   [r_wss, r_act])
            return ps[0:ncols, 0:TT], r_ps
        t, r = load_slab(l, wab_at(l, slab_idx), KC)
        for kc in range(KC):
            mm(ps[0:128, 0:TT], r_ps, t[:, kc, :], act_t[:, kc, :], kc == 0, kc == KC - 1, [r, r_act])
        return ps[0:128, 0:TT], r_ps

    SC = DK ** -0.5

    class StopBuild(Exception):
        pass

    def chk(stage):
        if getattr(cfg, "stop", None) == stage:
            raise StopBuild()

    def rms_prologue(l, gbase):
        for kc in range(KC):
            s, rs = sq[kc % 2], r_sq[kc % 2]
            act(s[:, :], hT[:, kc, :], AF.Square, [r_hT], [rs])
            mm(SSB[:, 0:TT], r_SSB, ones_bf[:, :], s[:, :], kc == 0, kc == KC - 1, [rs, r_par])
            gcol = pvec[:, gbase + kc:gbase + kc + 1]
            ts("pool", xg[:, kc, :], hT[:, kc, :], gcol, None, ALU.mult, None, [r_hT, r_par], [r_xg])
        act(rstd[:, :], SSB[:, 0:TT], AF.Sqrt, [r_SSB], [r_rstd], bias=EPS, scale=1.0 / D)
        tr.op("dve", lambda e: e.reciprocal(out=rstd[:, :], in_=rstd[:, :]), reads=[r_rstd], writes=[r_rstd])

    def logsigmoid(out_ap, x_ap, t1, t2, reads, writes, r_t, post=-1.0):
        act(t1, x_ap, AF.Abs, reads, [r_t])
        act(t1, t1, AF.Exp, [r_t], [r_t], scale=-1.0)
        act(t1, t1, AF.Ln, [r_t], [r_t], bias=1.0)
        act(t2, x_ap, AF.Relu, reads, [r_t], scale=-1.0)
        tt("dve", t2, t2, t1, ALU.add, [r_t], [r_t])
        ts("dve", out_ap, t2, post, None, ALU.mult, None, [r_t], writes)

    try:
      tr.dma("sp", "par", lambda e: e.dma_start(out=pvec[:, :], in_=pvec_d[:, :]), writes=[r_par])
      tr.dma("sp", "par", lambda e: e.dma_start(out=bif[:, :], in_=bif_d[:, :]), writes=[r_par])
      tr.dma("sp", "par", lambda e: e.dma_start(out=wg[:, :], in_=wg_d[:, :]), writes=[r_par])
      tr.dma("sp", "par", lambda e: e.dma_start(out=cst[:, :], in_=cst_d[:, :]), writes=[r_par])
      chk(-1)
      cp("act", ones_bf[:, :], ONES, [r_par], [r_par])
      memset("dve", vA[:, DV:DV + 1], 1.0, [r_vA])
      memset("dve", lnsc[:, 0:1], math.log(SC), [r_par])

      chk(-2)
      cast_layer(0)

      for l in range(DEPTH):
        chk(0)
        if l + 1 < DEPTH:
            cast_layer(l + 1)
        tr.dma("sp", "wss", lambda e, l=l: e.dma_start(out=wss[:, :, :], in_=wsb[l]), reads=[r_wl[l]], writes=[r_wss])
        tr.dma("sp", "bg", lambda e, l=l: e.dma_start(out=bgbc[:, :], in_=bg_d[:, l * 512:(l + 1) * 512]), writes=[r_bg])
        for h in range(H):
            memset("dve", rawq[h][:, 0:3], 0.0, [r_rawq[h]])
            memset("dve", rawk[h][:, 0:3], 0.0, [r_rawk[h]])
            memset("dve", CT[h][:, :], 0.0, [r_CT[h]])
            memset("dve", ST[h][:, :], 0.0, [r_ST[h]])
            memset("dve", CTb[h][:, :], 0.0, [r_CTb[h]])
            memset("dve", STb[h][:, :], 0.0, [r_STb[h]])

        for ti in range(NT):
            src = xin if l == 0 else hT_d
            r_src = r_hd[ti] if l > 0 else None
            tr.dma("act", "hld", lambda e, src=src, ti=ti: e.dma_start(out=hT[:, :, :], in_=src[ti]),
                   reads=([r_src] if r_src is not None else []), writes=[r_hT])
            chk(1)
            rms_prologue(l, cfg.P_G1 + l * KC)
            chk(2)
            ps, r_ps = gemm(l, None, xg, r_xg, small=(0, 8))
            tt("dve", ifs[:, :], ps, rstd[0:8, :], ALU.mult, [r_ps, r_rstd], [r_ifs])
            ts("dve", ifs[:, :], ifs[:, :], bif[:, l:l + 1], None, ALU.add, None, [r_ifs, r_par], [r_ifs])
            logsigmoid(lsg[:, :], ifs[:, :], ift[0][:, :], ift[1][:, :], [r_ifs], [r_lsg], r_ift)
            ps, r_ps = gemm(l, None, xg, r_xg, small=(8, 16))
            tt("dve", zT[:, :], ps, rstd[0:16, :], ALU.mult, [r_ps, r_rstd], [r_zT])

            chk(3)
            for h in range(H):
                def proj(slab, dst_ap, r_dst, func=None):
                    ps, r_ps = gemm(l, slab, xg, r_xg)
                    if func is None:
                        tt("dve", dst_ap, ps, rstd[:, :], ALU.mult, [r_ps, r_rstd], [r_dst])
                    else:
                        tt("dve", dst_ap, ps, rstd[:, :], ALU.mult, [r_ps, r_rstd], [r_dst])
                        act(dst_ap, dst_ap, func, [r_dst], [r_dst])
                proj(cfg.S_QA + h, rawq[h][:, 3:3 + TT], r_rawq[h])
                proj(cfg.S_KA + h, rawk[h][:, 3:3 + TT], r_rawk[h])
                proj(cfg.S_VA + 2 * h, stg["va0"][:, :], r_stg["va0"])
                proj(cfg.S_VA + 2 * h + 1, stg["va1"][:, :], r_stg["va1"])
                proj(cfg.S_OA + 2 * h, stg["oa0"][:, :], r_stg["oa0"], AF.Sigmoid)
                proj(cfg.S_OA + 2 * h + 1, stg["oa1"][:, :], r_stg["oa1"], AF.Sigmoid)
                proj(cfg.S_QB + h, stg["qb"][:, :], r_stg["qb"])
                proj(cfg.S_KB + h, stg["kb"][:, :], r_stg["kb"])
                proj(cfg.S_VB + 2 * h, stg["vb0"][:, :], r_stg["vb0"])
                proj(cfg.S_VB + 2 * h + 1, stg["vb1"][:, :], r_stg["vb1"])
                proj(cfg.S_GB + 2 * h, stg["gb0"][:, :], r_stg["gb0"], AF.Silu)
                proj(cfg.S_GB + 2 * h + 1, stg["gb1"][:, :], r_stg["gb1"], AF.Silu)

                chk(4)
                for which, raw, r_raw in (("q", rawq[h], r_rawq[h]), ("k", rawk[h], r_rawk[h])):
                    blk = h if which == "q" else 4 + h
                    cb = cfg.P_CONV + (l * 8 + blk) * 4
                    ts("dve", cacc[:, :], raw[:, 0:TT], pvec[:, cb:cb + 1], None, ALU.mult, None,
                       [r_raw, r_par], [r_cacc])
                    for j in range(1, 4):
                        stt("dve", cacc[:, :], raw[:, j:j + TT], pvec[:, cb + j:cb + j + 1], cacc[:, :],
                            ALU.mult, ALU.add, [r_raw, r_par, r_cacc], [r_cacc])
                    cp("pool", raw[:, 0:3], raw[:, TT:TT + 3], [r_raw, r_cacc], [r_raw])
                    if which == "q":
                        act(qa_bf[:, :], cacc[:, :], AF.Silu, [r_cacc], [r_qabf])
                    else:
                        act(ka_f[:, :], cacc[:, :], AF.Silu, [r_cacc], [r_kaf])
                        ts("dve", ka_f[:, :], ka_f[:, :], SC, None, ALU.mult, None, [r_kaf], [r_kaf])
                        cp("pool", ka_bf[:, :], ka_f[:, :], [r_kaf], [r_kabf])

                chk(5)
                for c in range(NCH):
                    cs = slice(c * L, (c + 1) * L)
                    pg, r_pg = mbank()
                    mm(pg[0:L, 0:8], r_pg, ifs[:, cs], IDENT[0:8, 0:8], True, True, [r_ifs, r_par])
                    mm(pg[0:L, 8:16], r_pg, lsg[:, cs], IDENT[0:8, 0:8], True, True, [r_lsg, r_par])
                    cp("dve", colsA[0:L, 0:16], pg[0:L, 0:16], [r_pg], [r_colsA])
                    li = colsA[0:L, h:h + 1]
                    lf = colsA[0:L, 12 + h:13 + h]
                    pb, r_pb = mbank()
                    mm(pb[0:L, 0:1], r_pb, TRI[0:L, 0:L], lf, True, True, [r_colsA, r_par])
                    mm(pb[0:128, 4:5], r_pb, ONES[0:L, 0:128], lf, True, True, [r_colsA, r_par])
                    tt("dve", b4[0:L, 0:1], li, pb[0:L, 0:1], ALU.subtract, [r_colsA, r_pb], [r_b4])
                    act(b4[0:L, 4:5], pb[0:L, 0:1], AF.Exp, [r_pb], [r_b4])
                    cp("dve", g4[:, 0:1], pb[0:128, 4:5], [r_pb], [r_g4])
                    act(g4[:, 4:5], pb[0:128, 4:5], AF.Exp, [r_pb], [r_g4])
                    act(b4[0:L, 8:9], b4[0:L, 0:1], AF.Exp, [r_b4, r_g4], [r_b4], bias=g4[0:L, 0:1])
                    ts("dve", lfbc[0:L, 0:L], ONES[0:L, 0:L], lf, None, ALU.mult, None, [r_colsA, r_par], [r_lfbc])
                    pB, r_pB = mbank()
                    mm(pB[0:L, 0:L], r_pB, lfbc[0:L, 0:L], TRI[0:L, 0:L], True, True, [r_lfbc, r_par])
                    tt("dve", Bm[0:L, 0:L], pB[0:L, 0:L], MASKNEG[0:L, 0:L], ALU.add, [r_pB, r_par], [r_Bm])
                    act(DTt[0:L, 0:L], Bm[0:L, 0:L], AF.Exp, [r_Bm, r_b4], [r_DT], bias=b4[0:L, 0:1])
                    chk(6)
                    pk, r_pk = mbank()
                    mm(pk[0:L, 0:L], r_pk, ka_bf[:, cs], qa_bf[:, cs], True, True, [r_kabf, r_qabf])
                    chk(6.05)
                    tt("dve", Bm[0:L, 0:L], pk[0:L, 0:L], DTt[0:L, 0:L], ALU.mult, [r_pk, r_DT, r_Bm], [r_Bm])
                    cp("act", STt[0:L, 0:L], Bm[0:L, 0:L], [r_Bm], [r_STt])
                    chk(6.1)
                    pv, r_pv = mbank()
                    mm(pv[0:L, 0:128], r_pv, stg["va0"][:, cs], IDENT, True, True, [r_stg["va0"], r_par])
                    mm(pv[0:L, 128:256], r_pv, stg["va1"][:, cs], IDENT, True, True, [r_stg["va1"], r_par])
                    mm(pv[0:L, 256:384], r_pv, ka_f[:, cs], IDENT, True, True, [r_kaf, r_par])
                    chk(6.15)
                    cp("act", vA[0:L, 0:DV], pv[0:L, 0:DV], [r_pv], [r_vA])
                    chk(6.17)
                    act(kw[0:L, :], pv[0:L, 256:384], AF.Identity, [r_pv, r_b4], [r_kw], scale=b4[0:L, 8:9])
                    chk(6.2)
                    po, r_po = mbank()
                    mm(po[0:L, 0:DV + 1], r_po, STt[0:L, 0:L], vA[0:L, :], True, True, [r_STt, r_vA])
                    pc, r_pc = mbank()
                    mm(pc[0:L, 0:DV + 1], r_pc, qa_bf[:, cs], CTb[h][:, :], True, True, [r_qabf, r_CTb[h]])
                    chk(6.3)
                    act(tmpc[0:L, :], pc[0:L, 0:DV + 1], AF.Identity, [r_pc, r_b4], [r_tmpc], scale=b4[0:L, 4:5])
                    chk(6.4)
                    tt("dve", nd[0:L, :], tmpc[0:L, :], po[0:L, 0:DV + 1], ALU.add, [r_tmpc, r_po], [r_nd])
                    act(rr[0:L, 0:1], nd[0:L, DV:DV + 1], AF.Abs, [r_nd], [r_rr])
                    ts("dve", rr[0:L, 0:1], rr[0:L, 0:1], 1.0, None, ALU.max, None, [r_rr], [r_rr])
                    chk(6.5)
                    tr.op("dve", lambda e: e.reciprocal(out=rr[0:L, 1:2], in_=rr[0:L, 0:1]), reads=[r_rr], writes=[r_rr])
                    ts("dve", ha[0:L, :], nd[0:L, 0:DV], rr[0:L, 1:2], None, ALU.mult, None, [r_nd, r_rr], [r_ha])
                    chk(7)
                    pt, r_pt = mbank()
                    mm(pt[0:128, 0:L], r_pt, ha[0:L, 0:128], IDENT[0:L, 0:L], True, True, [r_ha, r_par])
                    mm(pt[0:128, 128:128 + L], r_pt, ha[0:L, 128:256], IDENT[0:L, 0:L], True, True, [r_ha, r_par])
                    cp("act", lt3[:, 0:L], pt[0:128, 0:L], [r_pt], [r_lt3])
                    tt("pool", yT[:, 2 * h, cs], lt3[:, 0:L], stg["oa0"][:, cs], ALU.mult, [r_lt3, r_stg["oa0"]], [r_yT])
                    cp("act", lt4[:, 0:L], pt[0:128, 128:128 + L], [r_pt], [r_lt4])
                    tt("pool", yT[:, 2 * h + 1, cs], lt4[:, 0:L], stg["oa1"][:, cs], ALU.mult,
                       [r_lt4, r_stg["oa1"]], [r_yT])
                    pu, r_pu = mbank()
                    mm(pu[0:128, 0:DV + 1], r_pu, kw[0:L, :], vA[0:L, :], True, True, [r_kw, r_vA])
                    stt("dve", CT[h][:, :], CT[h][:, :], g4[:, 4:5], pu[0:128, 0:DV + 1], ALU.mult, ALU.add,
                        [r_CT[h], r_g4, r_pu], [r_CT[h]])
                    cp("act", CTb[h][:, :], CT[h][:, :], [r_CT[h]], [r_CTb[h]])

                    chk(8)
                    wgh = wg[:, l * 512 + h * 128:l * 512 + (h + 1) * 128]
                    pl, r_pl = mbank()
                    mm(pl[0:L, 0:128], r_pl, zT[:, cs], wgh, True, True, [r_zT, r_par])
                    tt("dve", lt1[0:L, :], pl[0:L, 0:128], bgbc[0:L, h * 128:(h + 1) * 128], ALU.add,
                       [r_pl, r_bg], [r_lt1])
                    logsigmoid(la[0:L, :], lt1[0:L, :], lt2[0:L, :], lt5[0:L, :], [r_lt1], [r_la], r_lt2, post=-1.0 / 16.0)
                    chk(9)
                    pq, r_pq = mbank()
                    mm(pq[0:L, 0:128], r_pq, TRI[0:L, 0:L], la[0:L, :], True, True, [r_la, r_par])
                    mm(pq[0:L, 128:256], r_pq, ONES[0:L, 0:L], la[0:L, :], True, True, [r_la, r_par])
                    mm(pq[0:128, 256:256 + L], r_pq, la[0:L, :], TRI[0:L, 0:L], True, True, [r_la, r_par])
                    cp("dve", cum_sb[0:L, :], pq[0:L, 0:128], [r_pq], [r_cum])
                    tt("dve", edec[0:L, :], pq[0:L, 128:256], cum_sb[0:L, :], ALU.subtract, [r_pq, r_cum], [r_edec])
                    act(edec[0:L, :], edec[0:L, :], AF.Exp, [r_edec], [r_edec])
                    act(ecT[:, 0:L], pq[0:128, 256:256 + L], AF.Exp, [r_pq, r_par], [r_ecT], bias=lnsc[:, 0:1])
                    act(encT[:, 0:L], pq[0:128, 256:256 + L], AF.Exp, [r_pq], [r_encT], scale=-1.0)
                    act(elast[:, 0:1], pq[0:128, 256 + L - 1:256 + L], AF.Exp, [r_pq], [r_elast])
                    tt("pool", qt[:, 0:L], stg["qb"][:, cs], ecT[:, 0:L], ALU.mult, [r_stg["qb"], r_ecT], [r_qt])
                    tt("pool", kt[:, 0:L], stg["kb"][:, cs], encT[:, 0:L], ALU.mult, [r_stg["kb"], r_encT], [r_kt])
                    pa, r_pa = mbank()
                    mm(pa[0:L, 0:L], r_pa, kt[:, 0:L], qt[:, 0:L], True, True, [r_kt, r_qt])
                    cp("act", lt3[0:L, 0:L], pa[0:L, 0:L], [r_pa], [r_lt3])
                    tt("pool", attT[0:L, 0:L], lt3[0:L, 0:L], TRI[0:L, 0:L], ALU.mult, [r_lt3, r_par], [r_attT])
                    pv2, r_pv2 = mbank()
                    mm(pv2[0:L, 0:128], r_pv2, stg["vb0"][:, cs], IDENT, True, True, [r_stg["vb0"], r_par])
                    mm(pv2[0:L, 128:256], r_pv2, stg["vb1"][:, cs], IDENT, True, True, [r_stg["vb1"], r_par])
                    mm(pv2[0:L, 256:384], r_pv2, stg["kb"][:, cs], IDENT, True, True, [r_stg["kb"], r_par])
                    cp("act", vB[0:L, :], pv2[0:L, 0:DV], [r_pv2], [r_vB])
                    cp("act", lt4[0:L, :], pv2[0:L, 256:384], [r_pv2], [r_lt4])
                    tt("pool", kd[0:L, :], lt4[0:L, :], edec[0:L, :], ALU.mult, [r_lt4, r_edec], [r_kd])
                    po2, r_po2 = mbank()
                    mm(po2[0:L, 0:DV], r_po2, attT[0:L, 0:L], vB[0:L, :], True, False, [r_attT, r_vB])
                    mm(po2[0:L, 0:DV], r_po2, qt[:, 0:L], STb[h][:, :], False, True, [r_qt, r_STb[h]])
                    chk(10)
                    cp("dve", on[0:L, :], po2[0:L, 0:DV], [r_po2], [r_on])
                    memset("dve", ssb[0:L, 0:1], 0.0, [r_ssb])
                    act(junk[0:L, :], on[0:L, :], AF.Square, [r_on], [r_junk, r_ssb], accum_out=ssb[0:L, 0:1])
                    act(ssb[0:L, 1:2], ssb[0:L, 0:1], AF.Sqrt, [r_ssb], [r_ssb], bias=EPS, scale=1.0 / DV)
                    tr.op("dve", lambda e: e.reciprocal(out=ssb[0:L, 1:2], in_=ssb[0:L, 1:2]), reads=[r_ssb], writes=[r_ssb])
                    ts("dve", on[0:L, :], on[0:L, :], ssb[0:L, 1:2], None, ALU.mult, None, [r_on, r_ssb], [r_on])
                    pt2, r_pt2 = mbank()
                    mm(pt2[0:128, 0:L], r_pt2, on[0:L, 0:128], IDENT[0:L, 0:L], True, True, [r_on, r_par])
                    mm(pt2[0:128, 128:128 + L], r_pt2, on[0:L, 128:256], IDENT[0:L, 0:L], True, True, [r_on, r_par])
                    ngb = cfg.P_NG + l * 8 + 2 * h
                    act(lt3[:, 0:L], pt2[0:128, 0:L], AF.Identity, [r_pt2, r_par], [r_lt3], scale=pvec[:, ngb:ngb + 1])
                    tt("pool", yT[:, 8 + 2 * h, cs], lt3[:, 0:L], stg["gb0"][:, cs], ALU.mult, [r_lt3, r_stg["gb0"]], [r_yT])
                    act(lt4[:, 0:L], pt2[0:128, 128:128 + L], AF.Identity, [r_pt2, r_par], [r_lt4], scale=pvec[:, ngb + 1:ngb + 2])
                    tt("pool", yT[:, 8 + 2 * h + 1, cs], lt4[:, 0:L], stg["gb1"][:, cs], ALU.mult,
                       [r_lt4, r_stg["gb1"]], [r_yT])
                    pu2, r_pu2 = mbank()
                    mm(pu2[0:128, 0:DV], r_pu2, kd[0:L, :], vB[0:L, :], True, True, [r_kd, r_vB])
                    stt("dve", ST[h][:, :], ST[h][:, :], elast[:, 0:1], pu2[0:128, 0:DV], ALU.mult, ALU.add,
                        [r_ST[h], r_elast, r_pu2], [r_ST[h]])
                    cp("act", STb[h][:, :], ST[h][:, :], [r_ST[h]], [r_STb[h]])

            chk(11)
            for d in range(KC):
                ps1, r1 = gemm(l, cfg.S_GA + d, xg, r_xg)
                tt("dve", tmpA[0][:, :], ps1, rstd[:, :], ALU.mult, [r1, r_rstd], [r_tmpA[0]])
                act(tmpA[0][:, :], tmpA[0][:, :], AF.Sigmoid, [r_tmpA[0]], [r_tmpA[0]])
                ps2, r2 = gemm(l, cfg.S_GBR + d, xg, r_xg)
                tt("dve", tmpA[1][:, :], ps2, rstd[:, :], ALU.mult, [r2, r_rstd], [r_tmpA[1]])
                act(tmpA[1][:, :], tmpA[1][:, :], AF.Sigmoid, [r_tmpA[1]], [r_tmpA[1]])
                t, r = load_slab(l, wbb[l * KC + d], 16)
                ps3, r3 = gbank()
                for kc in range(8):
                    mm(ps3[0:128, 0:TT], r3, t[:, kc, :], yT[:, kc, :], kc == 0, kc == 7, [r, r_yT])
                ps4, r4 = gbank()
                for kc in range(8):
                    mm(ps4[0:128, 0:TT], r4, t[:, 8 + kc, :], yT[:, 8 + kc, :], kc == 0, kc == 7, [r, r_yT])
                tt("dve", tmpA[0][:, :], ps3[0:128, 0:TT], tmpA[0][:, :], ALU.mult, [r3, r_tmpA[0]], [r_tmpA[0]])
                tt("dve", tmpA[1][:, :], ps4[0:128, 0:TT], tmpA[1][:, :], ALU.mult, [r4, r_tmpA[1]], [r_tmpA[1]])
                tt("pool", a2[:, d, :], tmpA[0][:, :], tmpA[1][:, :], ALU.add, [r_tmpA[0], r_tmpA[1]], [r_a2])

            chk(12)
            g2b = cfg.P_G2 + l * KC
            for d in range(KC):
                ps, r_ps = gemm(l, cfg.S_OUT + d, a2, r_a2)
                tt("dve", hT[:, d, :], hT[:, d, :], ps, ALU.add, [r_hT, r_ps], [r_hT])
            for kc in range(KC):
                s, rs = sq[kc % 2], r_sq[kc % 2]
                act(s[:, :], hT[:, kc, :], AF.Square, [r_hT], [rs])
                mm(SSB[:, 0:TT], r_SSB, ones_bf[:, :], s[:, :], kc == 0, kc == KC - 1, [rs, r_par])
                ts("pool", xg[:, kc, :], hT[:, kc, :], pvec[:, g2b + kc:g2b + kc + 1], None, ALU.mult, None,
                   [r_hT, r_par], [r_xg])
            act(rstd[:, :], SSB[:, 0:TT], AF.Sqrt, [r_SSB], [r_rstd], bias=EPS, scale=1.0 / D)
            tr.op("dve", lambda e: e.reciprocal(out=rstd[:, :], in_=rstd[:, :]), reads=[r_rstd], writes=[r_rstd])

            chk(13)
            for q in range(4):
                for j in range(KC):
                    ps, r_ps = gemm(l, cfg.S_UP + q * KC + j, xg, r_xg)
                    tt("dve", tmpA[2][:, :], ps, rstd[:, :], ALU.mult, [r_ps, r_rstd], [r_tmpA[2]])
                    act(tmpA[3][:, :], tmpA[2][:, :], AF.Relu, [r_tmpA[2]], [r_tmpA[3]])
                    tt("pool", a2[:, j, :], tmpA[3][:, :], tmpA[3][:, :], ALU.mult, [r_tmpA[3]], [r_a2])
                for d in range(KC):
                    ps, r_ps = gemm(l, cfg.S_DN + q * KC + d, a2, r_a2)
                    tt("dve", hT[:, d, :], hT[:, d, :], ps, ALU.add, [r_hT, r_ps], [r_hT])

            chk(14)
            if l + 1 < DEPTH:
                if l == 0 and ti == 0:
                    pass
                tr.dma("act", "hst", lambda e, ti=ti: e.dma_start(out=hT_d[ti], in_=hT[:, :, :]),
                       reads=[r_hT], writes=[r_hd[ti]])
            else:
                for kc in range(KC):
                    s, rs = sq[kc % 2], r_sq[kc % 2]
                    act(s[:, :], hT[:, kc, :], AF.Square, [r_hT], [rs])
                    mm(SSB[:, 0:TT], r_SSB, ones_bf[:, :], s[:, :], kc == 0, kc == KC - 1, [rs, r_par])
                act(rstd[:, :], SSB[:, 0:TT], AF.Sqrt, [r_SSB], [r_rstd], bias=EPS, scale=1.0 / D)
                tr.op("dve", lambda e: e.reciprocal(out=rstd[:, :], in_=rstd[:, :]), reads=[r_rstd], writes=[r_rstd])
                for kc in range(KC):
                    stt("dve", hT[:, kc, :], hT[:, kc, :], pvec[:, cfg.P_GF + kc:cfg.P_GF + kc + 1], rstd[:, :],
                        ALU.mult, ALU.mult, [r_hT, r_par, r_rstd], [r_hT])
                tr.dma("act", "hst", lambda e, ti=ti: e.dma_start(out=out[ti], in_=hT[:, :, :]),
                       reads=[r_hT], writes=[r_od])

    except StopBuild:
        tr.dma("act", "hst", lambda e: e.dma_start(out=out[0], in_=hT[:, :, :]), reads=[r_hT], writes=[r_od])

    with nc.Block() as block:
        @block.tensor
        def _(e):
            tr.replay("pe", e)

        @block.vector
        def _(e):
            tr.replay("dve", e)

        @block.gpsimd
        def _(e):
            tr.replay("pool", e)

        @block.sync
        def _(e):
            tr.replay("sp", e)

        @block.scalar
        def _(e):
            tr.replay("act", e, extra_waits=tr.final_waits("act"))
    stack.close()
    return nc


def make_consts():
    c = np.zeros((128, 512), np.float32)
    c[:, 0:128] = np.eye(128, dtype=np.float32)
    tri = np.triu(np.ones((128, 128), np.float32))
    c[:, 128:256] = tri
    c[:, 256:384] = 1.0
    c[:, 384:512] = (1.0 - tri) * -30000.0
    return c


def _blk(w, KC):
    K, Nc = w.shape
    return np.ascontiguousarray(w.reshape(K // 128, 128, Nc // 128, 128).transpose(2, 1, 0, 3))


def prep_weights(cfg, inp):
    D, KC, DEPTH = cfg.D, cfg.KC, cfg.DEPTH
    was = []
    wb = np.empty((DEPTH * KC, 128, 16, 128), np.float32)
    ws = np.empty((DEPTH, 128, KC, 24), np.float32)
    for l in range(DEPTH):
        w_in = np.asarray(inp["w_in"][l])
        wa = np.empty((cfg.NSLAB, 128, KC, 128), np.float32)
        was.append(wa)
        b = 0
        wa[b + 0:b + 24] = _blk(w_in[:, 0:3072], KC)
        wa[b + 24:b + 48] = _blk(w_in[:, 3080:6152], KC)
        wa[b + 48:b + 48 + 2 * KC] = _blk(w_in[:, 6168:6168 + 2 * D], KC)
        wa[b + cfg.S_OUT:b + cfg.S_OUT + KC] = _blk(np.asarray(inp["w_out"][l]), KC)
        wa[b + cfg.S_UP:b + cfg.S_UP + 4 * KC] = _blk(np.asarray(inp["w_up"][l]), KC)
        wd = np.asarray(inp["w_down"][l])
        for q in range(4):
            wa[b + cfg.S_DN + q * KC:b + cfg.S_DN + (q + 1) * KC] = _blk(wd[q * D:(q + 1) * D, :], KC)
        wb[l * KC:(l + 1) * KC, :, 0:8, :] = _blk(np.asarray(inp["w_br_a"][l]), 8)
        wb[l * KC:(l + 1) * KC, :, 8:16, :] = _blk(np.asarray(inp["w_br_b"][l]), 8)
        sm = np.concatenate([w_in[:, 3072:3080], w_in[:, 6152:6168]], axis=1)
        ws[l] = sm.reshape(KC, 128, 24).transpose(1, 0, 2)
    pvec = np.zeros((128, cfg.NP), np.float32)
    def colsT(v):
        return np.asarray(v).reshape(-1, 128).T
    for l in range(DEPTH):
        pvec[:, cfg.P_G1 + l * KC:cfg.P_G1 + (l + 1) * KC] = colsT(inp["norm_mix"][l])
        pvec[:, cfg.P_G2 + l * KC:cfg.P_G2 + (l + 1) * KC] = colsT(inp["norm_mlp"][l])
        cv = np.asarray(inp["conv_qk"][l])
        for blk in range(8):
            for j in range(4):
                pvec[:, cfg.P_CONV + (l * 8 + blk) * 4 + j] = cv[j, blk * 128:(blk + 1) * 128]
        pvec[:, cfg.P_NG + l * 8:cfg.P_NG + (l + 1) * 8] = colsT(inp["norm_gla"][l])
    pvec[:, cfg.P_GF:cfg.P_GF + KC] = colsT(inp["norm_final"])
    bif = np.ascontiguousarray(np.asarray(inp["b_if"]).T.astype(np.float32))
    wg = np.ascontiguousarray(np.asarray(inp["w_gla_gate"]).transpose(1, 0, 2).reshape(16, DEPTH * 512))
    bg = np.asarray(inp["b_gla_gate"]).reshape(1, DEPTH * 512)
    bgbc = np.ascontiguousarray(np.broadcast_to(bg, (128, DEPTH * 512))).astype(np.float32)
    d = {f"wa{l}": was[l] for l in range(DEPTH)}
    return dict(d, wb=wb, ws=np.ascontiguousarray(ws), pvec=pvec, bif=bif, wg=wg, bgbc=bgbc, consts=make_consts())


def prep_x(cfg, x_b, meta):
    seq = np.concatenate([np.asarray(meta), np.asarray(x_b)], axis=0)
    a = seq.reshape(cfg.NT, cfg.TT, cfg.KC, 128).transpose(0, 3, 2, 1)
    return np.ascontiguousarray(a)


def unprep_out(cfg, o):
    return np.ascontiguousarray(o.transpose(0, 3, 2, 1)).reshape(cfg.NTOK, cfg.D)


def run(cfg, inp):
    import time, sys
    t0 = time.time()
    nc = build_program(cfg)
    print(f"[kernel] build {time.time() - t0:.1f}s", file=sys.stderr, flush=True)
    t0 = time.time()
    wts = prep_weights(cfg, inp)
    print(f"[kernel] prep {time.time() - t0:.1f}s", file=sys.stderr, flush=True)
    x = np.asarray(inp["x"])
    B = x.shape[0]
    assert B == cfg.NCORES
    in_maps = []
    for b in range(B):
        m = dict(wts)
        m["xin"] = prep_x(cfg, x[b], inp["meta"])
        in_maps.append(m)
    t0 = time.time()
    res = run_bass_kernel_spmd(nc, in_maps, core_ids=list(range(cfg.NCORES)))
    print(f"[kernel] launch {time.time() - t0:.1f}s", file=sys.stderr, flush=True)
    outs = [unprep_out(cfg, np.asarray(res.results[b]["out"]))[N_META:] for b in range(B)]
    return np.stack(outs, 0).astype(np.float32)


def kernel(**inputs):
    cfg = Cfg()
    return run(cfg, inputs)
```
